# Optimizing a Trainium2 kernel written in Bass

```python
import jax, jax.numpy as jnp
from jax import lax
import numpy as np

D_MODEL = 2048
BATCH = 4
SEQ = 4096
DEPTH = 4

CHUNK = 64
N_A_LAYERS = DEPTH // 2
N_B_LAYERS = DEPTH - N_A_LAYERS
EPS = 1e-6
RET_HEADS = 8
RET_QK_DIM = D_MODEL // RET_HEADS
RET_V_DIM = 2 * RET_QK_DIM
RET_QK_WIDTH = RET_HEADS * RET_QK_DIM
RET_V_WIDTH = RET_HEADS * RET_V_DIM
RET_IN_WIDTH = 2 * RET_QK_WIDTH + 2 * RET_V_WIDTH
ROPE_BASE = 10000.0
ATT_HEADS = 16
ATT_HEAD_DIM = D_MODEL // ATT_HEADS
LEFT_CHUNKS = 8
BAND = (LEFT_CHUNKS + 1) * CHUNK
REL_CLIP = 128
N_REL = 2 * REL_CLIP + 1
PEER_HEADS = 8
PEER_NKEYS = 128
PEER_EXPERTS = PEER_NKEYS * PEER_NKEYS
PEER_QDIM = 256
PEER_HALF = PEER_QDIM // 2
PEER_TOPK = 16
PEER_TOKEN_BLOCK = 128

kernel_name = "yoco_retention_chunkattn_peer"


def rmsnorm(x, g):
    xf = x.astype(jnp.float32)
    y = xf * lax.rsqrt(jnp.mean(xf * xf, axis=-1, keepdims=True) + EPS)
    return (y * g.astype(jnp.float32)).astype(x.dtype)


def rotary(t, pos):
    half = t.shape[-1] // 2
    inv = 1.0 / (ROPE_BASE ** (jnp.arange(half, dtype=jnp.float32) / half))
    ang = pos[:, None] * inv[None, :]
    cos = jnp.cos(ang)[None, :, None, :]
    sin = jnp.sin(ang)[None, :, None, :]
    tf = t.astype(jnp.float32)
    t1, t2 = tf[..., :half], tf[..., half:]
    return jnp.concatenate([t1 * cos - t2 * sin, t1 * sin + t2 * cos], axis=-1).astype(t.dtype)


def retention(h, w_in, w_out, gn_g):
    B, S, _ = h.shape
    NC = S // CHUNK
    dt = h.dtype
    proj = h @ w_in
    q, k, v, g = jnp.split(proj, [RET_QK_WIDTH, 2 * RET_QK_WIDTH, 2 * RET_QK_WIDTH + RET_V_WIDTH], axis=-1)
    pos = jnp.arange(S, dtype=jnp.float32)
    q = rotary(q.reshape(B, S, RET_HEADS, RET_QK_DIM), pos)
    k = rotary(k.reshape(B, S, RET_HEADS, RET_QK_DIM), pos) * (RET_QK_DIM ** -0.5)
    v = v.reshape(B, S, RET_HEADS, RET_V_DIM)
    qc = q.reshape(B, NC, CHUNK, RET_HEADS, RET_QK_DIM).transpose(0, 1, 3, 2, 4)
    kc = k.reshape(B, NC, CHUNK, RET_HEADS, RET_QK_DIM).transpose(0, 1, 3, 2, 4)
    vc = v.reshape(B, NC, CHUNK, RET_HEADS, RET_V_DIM).transpose(0, 1, 3, 2, 4)
    log_gamma = jnp.log1p(-jnp.exp2(-5.0 - jnp.arange(RET_HEADS, dtype=jnp.float32)))
    idx = jnp.arange(CHUNK, dtype=jnp.float32)
    diff = idx[:, None] - idx[None, :]
    decay_mask = jnp.where(diff >= 0, jnp.exp(log_gamma[:, None, None] * jnp.maximum(diff, 0.0)), 0.0).astype(dt)
    xi = jnp.exp(log_gamma[:, None] * (idx[None, :] + 1.0)).astype(dt)
    zeta = jnp.exp(log_gamma[:, None] * (CHUNK - 1.0 - idx[None, :])).astype(dt)
    chunk_decay = jnp.exp(log_gamma * CHUNK).astype(dt)
    scores = jnp.einsum('bnhid,bnhjd->bnhij', qc, kc) * decay_mask
    intra = jnp.einsum('bnhij,bnhje->bnhie', scores, vc)
    def step(R, inp):
        q_, kz, v_ = inp
        cross = jnp.einsum('bhcd,bhde->bhce', q_, R)
        R = R * chunk_decay[None, :, None, None] + jnp.einsum('bhcd,bhce->bhde', kz, v_)
        return R, cross
    xs = (qc.transpose(1, 0, 2, 3, 4),
          (kc * zeta[None, None, :, :, None]).transpose(1, 0, 2, 3, 4),
          vc.transpose(1, 0, 2, 3, 4))
    R0 = jnp.zeros((B, RET_HEADS, RET_QK_DIM, RET_V_DIM), dt)
    _, cross = lax.scan(step, R0, xs)
    cross = cross.transpose(1, 0, 2, 3, 4) * xi[None, None, :, :, None]
    o = (intra + cross).transpose(0, 1, 3, 2, 4).reshape(B, S, RET_HEADS, RET_V_DIM)
    of = o.astype(jnp.float32)
    mu = jnp.mean(of, axis=-1, keepdims=True)
    var = jnp.mean(jnp.square(of - mu), axis=-1, keepdims=True)
    y = ((of - mu) * lax.rsqrt(var + EPS)).reshape(B, S, RET_V_WIDTH) * gn_g.astype(jnp.float32)
    y = y.astype(dt)
    return (jax.nn.silu(g) * y) @ w_out


def shared_kv(x, g_kv, w_kv):
    B, S, _ = x.shape
    kv = rmsnorm(x, g_kv) @ w_kv
    k, v = jnp.split(kv, 2, axis=-1)
    pad = ((0, 0), (LEFT_CHUNKS * CHUNK, 0), (0, 0), (0, 0))
    k = jnp.pad(k.reshape(B, S, ATT_HEADS, ATT_HEAD_DIM), pad)
    v = jnp.pad(v.reshape(B, S, ATT_HEADS, ATT_HEAD_DIM), pad)
    return k, v


def chunk_attention(h, k_pad, v_pad, w_q, w_o, rel_bias):
    B, S, D = h.shape
    NC = S // CHUNK
    q = (h @ w_q).reshape(B, NC, CHUNK, ATT_HEADS, ATT_HEAD_DIM) * (ATT_HEAD_DIM ** -0.5)
    q = q.transpose(1, 0, 2, 3, 4)
    i_idx = jnp.arange(CHUNK)[:, None]
    m_idx = jnp.arange(BAND)
    rel = jnp.clip(LEFT_CHUNKS * CHUNK + i_idx - m_idx[None, :], -REL_CLIP, REL_CLIP) + REL_CLIP
    bias = rel_bias.astype(jnp.float32)[:, rel]

    def one_chunk(args):
        c, qc = args
        kb = lax.dynamic_slice_in_dim(k_pad, c * CHUNK, BAND, axis=1)
        vb = lax.dynamic_slice_in_dim(v_pad, c * CHUNK, BAND, axis=1)
        s = jnp.einsum('bihd,bjhd->bhij', qc, kb).astype(jnp.float32) + bias
        valid = m_idx >= (LEFT_CHUNKS - c) * CHUNK
        s = jnp.where(valid[None, None, None, :], s, -1e30)
        p = jax.nn.softmax(s, axis=-1).astype(vb.dtype)
        return jnp.einsum('bhij,bjhd->bihd', p, vb)

    out = lax.map(one_chunk, (jnp.arange(NC), q))
    out = out.transpose(1, 0, 2, 3, 4).reshape(B, S, D)
    return out @ w_o


def peer(h, w_q, sub_keys, u, v):
    B, S, D = h.shape
    T = B * S
    hf = h.reshape(T, D)
    q = (hf @ w_q).reshape(T, PEER_HEADS, 2, PEER_HALF)
    s = jnp.einsum('thpd,pkd->thpk', q, sub_keys).astype(jnp.float32)
    s1, i1 = lax.top_k(s[:, :, 0], PEER_TOPK)
    s2, i2 = lax.top_k(s[:, :, 1], PEER_TOPK)
    cand_s = (s1[..., :, None] + s2[..., None, :]).reshape(T, PEER_HEADS, PEER_TOPK * PEER_TOPK)
    cand_i = (i1[..., :, None] * PEER_NKEYS + i2[..., None, :]).reshape(T, PEER_HEADS, PEER_TOPK * PEER_TOPK)
    top_s, pos = lax.top_k(cand_s, PEER_TOPK)
    eidx = jnp.take_along_axis(cand_i, pos, axis=-1)
    w = jax.nn.softmax(top_s, axis=-1).astype(h.dtype)
    nblk = T // PEER_TOKEN_BLOCK
    hb = hf.reshape(nblk, PEER_TOKEN_BLOCK, D)
    eb = eidx.reshape(nblk, PEER_TOKEN_BLOCK, PEER_HEADS * PEER_TOPK)
    wb = w.reshape(nblk, PEER_TOKEN_BLOCK, PEER_HEADS * PEER_TOPK)

    def block(args):
        hb_, idx_, w_ = args
        ug = jnp.take(u, idx_, axis=0)
        a = jax.nn.gelu(jnp.einsum('td,ted->te', hb_, ug), approximate=False) * w_
        vg = jnp.take(v, idx_, axis=0)
        return jnp.einsum('te,ted->td', a, vg)

    out = lax.map(block, (hb, eb, wb))
    return out.reshape(B, S, D)


def setup_inputs(seed: int = 0) -> dict:
    key = jax.random.key(seed)
    ks = jax.random.split(key, 16)
    f32 = jnp.float32
    D = D_MODEL
    nrm = lambda k, shape, scale: jax.random.normal(k, shape, f32) * scale
    return {
        "x": nrm(ks[0], (BATCH, SEQ, D), 1.0),
        "ln_mix": 1.0 + nrm(ks[1], (DEPTH, D), 0.02),
        "ln_ffn": 1.0 + nrm(ks[2], (DEPTH, D), 0.02),
        "ret_w_in": nrm(ks[3], (N_A_LAYERS, D, RET_IN_WIDTH), D ** -0.5),
        "ret_w_out": nrm(ks[4], (N_A_LAYERS, RET_V_WIDTH, D), RET_V_WIDTH ** -0.5),
        "ret_gn": 1.0 + nrm(ks[5], (N_A_LAYERS, RET_V_WIDTH), 0.02),
        "kv_norm": 1.0 + nrm(ks[6], (D,), 0.02),
        "w_kv": nrm(ks[7], (D, 2 * D), D ** -0.5),
        "att_w_q": nrm(ks[8], (N_B_LAYERS, D, D), D ** -0.5),
        "att_w_o": nrm(ks[9], (N_B_LAYERS, D, D), D ** -0.5),
        "att_rel_bias": nrm(ks[10], (N_B_LAYERS, ATT_HEADS, N_REL), 0.1),
        "peer_w_q": nrm(ks[11], (DEPTH, D, PEER_HEADS * PEER_QDIM), D ** -0.5),
        "peer_sub_keys": nrm(ks[12], (DEPTH, 2, PEER_NKEYS, PEER_HALF), PEER_HALF ** -0.5),
        "peer_u": nrm(ks[13], (DEPTH, PEER_EXPERTS, D), D ** -0.5),
        "peer_v": nrm(ks[14], (DEPTH, PEER_EXPERTS, D), (PEER_HEADS * PEER_TOPK) ** -0.5),
        "ln_final": 1.0 + nrm(ks[15], (D,), 0.02),
    }


def reference(x, ln_mix, ln_ffn, ret_w_in, ret_w_out, ret_gn, kv_norm, w_kv,
              att_w_q, att_w_o, att_rel_bias, peer_w_q, peer_sub_keys, peer_u, peer_v,
              ln_final):
    k_pad = None
    v_pad = None
    for l in range(DEPTH):
        h = rmsnorm(x, ln_mix[l])
        if l < N_A_LAYERS:
            x = x + retention(h, ret_w_in[l], ret_w_out[l], ret_gn[l])
        else:
            j = l - N_A_LAYERS
            x = x + chunk_attention(h, k_pad, v_pad, att_w_q[j], att_w_o[j], att_rel_bias[j])
        x = x + peer(rmsnorm(x, ln_ffn[l]), peer_w_q[l], peer_sub_keys[l], peer_u[l], peer_v[l])
        if l == N_A_LAYERS - 1:
            k_pad, v_pad = shared_kv(x, kv_norm, w_kv)
    return rmsnorm(x, ln_final)
```

```python
import numpy as np
from contextlib import ExitStack
import ml_dtypes
import concourse.bass as bass
import concourse.mybir as mybir
from concourse.bass_utils import run_bass_kernel_spmd

F32 = mybir.dt.float32
BF16 = mybir.dt.bfloat16
ALU = mybir.AluOpType
AF = mybir.ActivationFunctionType
AX = mybir.AxisListType

D = 2048
NCH = D // 128
EPS = 1e-6
NEG = -1.0e30


class Dep:
    __slots__ = ("w", "r", "dsem", "dcnt")

    def __init__(self):
        self.w = None
        self.r = {}
        self.dsem = None
        self.dcnt = 0


class KB:
    def __init__(self, nc, es):
        self.nc = nc
        self.es = es
        self.eng = {"pe": nc.tensor, "dve": nc.vector, "act": nc.scalar,
                    "pool": nc.gpsimd, "sp": nc.sync}
        self.sem = {}
        self.cnt = {}
        for n in self.eng:
            self.sem[n] = es.enter_context(nc.semaphore("s_" + n))
            self.cnt[n] = 0
        self.waited = {n: {} for n in self.eng}
        self.dsems = []
        self.ninst = 0

    def sb(self, es, name, shape, dtype):
        self.nalloc = getattr(self, "nalloc", 0) + 1
        return es.enter_context(self.nc.sbuf_tensor("%s_%d" % (name, self.nalloc), list(shape), dtype))

    def ps(self, es, name, shape, dtype=F32):
        self.nalloc = getattr(self, "nalloc", 0) + 1
        return es.enter_context(self.nc.psum_tensor("%s_%d" % (name, self.nalloc), list(shape), dtype))

    def _waits(self, en, reads, writes):
        need = {}
        for d in reads:
            if d.w is not None:
                s, v = d.w
                if need.get(s, 0) < v:
                    need[s] = v
        for d in writes:
            if d.w is not None:
                s, v = d.w
                if need.get(s, 0) < v:
                    need[s] = v
            for s, v in d.r.items():
                if need.get(s, 0) < v:
                    need[s] = v
        e = self.eng[en]
        wd = self.waited[en]
        own = self.sem[en]
        for s, v in need.items():
            if en == "pe" and s is own:
                continue
            if wd.get(s, 0) >= v:
                continue
            e.wait_ge(s, v)
            wd[s] = v

    def op(self, en, fn, reads=(), writes=(), inc=True):
        self._waits(en, reads, writes)
        inst = fn()
        self.ninst += 1
        if inc:
            inst.then_inc(self.sem[en], 1)
            self.cnt[en] += 1
            tok = (self.sem[en], self.cnt[en])
        else:
            assert en == "pe"
            tok = (self.sem[en], self.cnt[en] + 1)
        s, v = tok
        for d in writes:
            d.w = tok
            d.r = {}
        for d in reads:
            if d.r.get(s, 0) < v:
                d.r[s] = v
        return inst

    def dma(self, q, out, in_, reads=(), writes=(), **kw):
        self._waits(q, reads, writes)
        dst = writes[0]
        if dst.dsem is None:
            dst.dsem = self.es.enter_context(self.nc.semaphore("d%d" % len(self.dsems)))
            self.dsems.append(dst)
        inst = self.eng[q].dma_start(out=out, in_=in_, **kw)
        self.ninst += 1
        dst.dcnt += 16
        inst.then_inc(dst.dsem, 16)
        tok = (dst.dsem, dst.dcnt)
        for d in writes:
            d.w = tok
            d.r = {}
        for d in reads:
            if d.r.get(tok[0], 0) < tok[1]:
                d.r[tok[0]] = tok[1]
        return inst

    def finish(self, deps, en="sp"):
        self._waits(en, deps, ())

    def barrier(self):
        for en, e in self.eng.items():
            wd = self.waited[en]
            for n2, s in self.sem.items():
                v = self.cnt[n2]
                if n2 == en or v == 0 or wd.get(s, 0) >= v:
                    continue
                e.wait_ge(s, v)
                wd[s] = v
            for d in self.dsems:
                if d.dcnt and wd.get(d.dsem, 0) < d.dcnt:
                    e.wait_ge(d.dsem, d.dcnt)
                    wd[d.dsem] = d.dcnt


def bc(ap, dims):
    part = list(ap.ap[0])
    return bass.AP(ap.tensor, ap.offset, [part] + [[s, c] for s, c in dims])


def make_ident(k, es):
    nc = k.nc
    identf = k.sb(es, "identf", [128, 128], F32)
    ident = k.sb(es, "ident", [128, 128], BF16)
    d = Dep()
    k.op("pool", lambda: nc.gpsimd.iota(identf[:], pattern=[[1, 128]], base=0, channel_multiplier=-1,
                                         allow_small_or_imprecise_dtypes=True), writes=[d])
    k.op("dve", lambda: nc.vector.tensor_scalar(ident[:], identf[:], 0.0, None, op0=ALU.is_equal),
         reads=[d], writes=[d])
    return ident, d


def rms_tile(k, xt, dx, gb, dgb, hn, dhn, junk, dj, ss, dss):
    nc = k.nc
    k.op("act", lambda: nc.scalar.activation(junk, xt, AF.Square, accum_out=ss), reads=[dx], writes=[dj, dss])
    k.op("dve", lambda: nc.vector.tensor_scalar(ss, ss, 1.0 / D, EPS, op0=ALU.mult, op1=ALU.add),
         reads=[dss], writes=[dss])
    k.op("act", lambda: nc.scalar.activation(ss, ss, AF.Sqrt), reads=[dss], writes=[dss])
    k.op("dve", lambda: nc.vector.reciprocal(ss, ss), reads=[dss], writes=[dss])
    k.op("dve", lambda: nc.vector.scalar_tensor_tensor(hn, xt, ss, gb, op0=ALU.mult, op1=ALU.mult),
         reads=[dx, dss, dgb], writes=[dhn])


def transpose_tile(k, hn, dhn, ident, did, pT, dpT, dst_fn, ddst, nch=NCH, eng="act"):
    nc = k.nc
    for half in range((nch + 7) // 8):
        p, dp = pT[half % len(pT)], dpT[half % len(pT)]
        n = min(8, nch - half * 8)
        for c in range(n):
            cc = half * 8 + c
            k.op("pe", lambda: nc.tensor.transpose(p[:, c, :], hn[:, cc * 128:(cc + 1) * 128], ident[:]),
                 reads=[dhn, did], writes=[dp])
        if eng == "act":
            k.op("act", lambda: nc.scalar.copy(dst_fn(half * 8, n), p[:, 0:n, :]), reads=[dp], writes=[ddst])
        else:
            k.op("dve", lambda: nc.vector.tensor_copy(dst_fn(half * 8, n), p[:, 0:n, :]), reads=[dp], writes=[ddst])


def build_post(KY, with_kv, final, NBLK=4, NSB=32):
    TOK = NBLK * 512
    KC = KY // 128
    nc = bass.Bass("TRN2", target_bir_lowering=False)
    x = nc.dram_tensor("x", [TOK, D], F32, kind="ExternalInput").ap()
    yT = nc.dram_tensor("yT", [KY, TOK], BF16, kind="ExternalInput").ap()
    wo = nc.dram_tensor("wo", [KY, D], F32, kind="ExternalInput").ap()
    lnf = nc.dram_tensor("lnf", [1, D], F32, kind="ExternalInput").ap()
    wq = nc.dram_tensor("wq", [D, D], F32, kind="ExternalInput").ap()
    keysT = nc.dram_tensor("keysT", [128, 2, 128], F32, kind="ExternalInput").ap()
    uT = nc.dram_tensor("uT", [32, 128, NCH * 512], F32, kind="ExternalInput").ap()
    vv = nc.dram_tensor("vv", [16384, D], F32, kind="ExternalInput").ap()
    xo = nc.dram_tensor("xo", [TOK, D], F32, kind="ExternalOutput").ap()
    if with_kv:
        kvn = nc.dram_tensor("kvn", [1, D], F32, kind="ExternalInput").ap()
        wkv = nc.dram_tensor("wkv", [D, 2 * D], F32, kind="ExternalInput").ap()
        kTo = nc.dram_tensor("kTo", [D, TOK], BF16, kind="ExternalOutput").ap()
        vo = nc.dram_tensor("vo", [TOK, D], BF16, kind="ExternalOutput").ap()
    if final:
        lnfin = nc.dram_tensor("lnfin", [1, D], F32, kind="ExternalInput").ap()
    vsb = vv.rearrange("(s b p) d -> s p b d", b=4, p=128)
    wo_r = wo.rearrange("(c p) n -> p c n", p=128)
    wq_r = wq.rearrange("(c p) n -> p c n", p=128)
    yT_r = yT.rearrange("(c p) t -> p c t", p=128)

    with ExitStack() as es:
        k = KB(nc, es)
        ident, did = make_ident(k, es)
        d_xo = Dep()
        pT = [k.ps(es, "pT%d" % i, [128, 8, 128], BF16) for i in range(2)]
        dpT = [Dep() for _ in range(2)]
        pF = [k.ps(es, "pF%d" % i, [128, 512], F32) for i in range(6)]
        dpF = [Dep() for _ in range(6)]

        with ExitStack() as ea:
            yTb = k.sb(ea, "yTb", [128, KC, 512], BF16)
            wob = [k.sb(ea, "wob%d" % i, [128, KC, 512], BF16) for i in range(2)]
            xt = [k.sb(ea, "xa%d" % i, [128, 512], F32) for i in range(4)]
            dy, dwob, dxt = Dep(), [Dep(), Dep()], [Dep() for _ in range(4)]
            it = 0
            for blk in range(NBLK):
                k.dma("sp", yTb[:], yT_r[:, :, blk * 512:(blk + 1) * 512], writes=[dy])
                for cb in range(4):
                    wb, dwb = wob[cb % 2], dwob[cb % 2]
                    k.dma("pool", wb[:], wo_r[:, :, cb * 512:(cb + 1) * 512], writes=[dwb])
                    for t in range(4):
                        r0 = blk * 512 + t * 128
                        xx, dxx = xt[it % 4], dxt[it % 4]
                        pp, dpp = pF[it % 2], dpF[it % 2]
                        it += 1
                        k.dma("sp", xx[:], x[r0:r0 + 128, cb * 512:(cb + 1) * 512], writes=[dxx])
                        for c in range(KC):
                            k.op("pe", lambda: nc.tensor.matmul(pp[:], yTb[:, c, t * 128:(t + 1) * 128], wb[:, c, :],
                                                                 start=(c == 0), stop=(c == KC - 1)),
                                 reads=[dy, dwb], writes=[dpp], inc=(c == KC - 1))
                        k.op("dve", lambda: nc.vector.tensor_tensor(xx[:], xx[:], pp[:], op=ALU.add),
                             reads=[dxx, dpp], writes=[dxx])
                        k.dma("sp", xo[r0:r0 + 128, cb * 512:(cb + 1) * 512], xx[:], reads=[dxx], writes=[d_xo])
        k.barrier()

        kT_sb = k.sb(es, "kT_sb", [128, 2, 128], BF16)
        dkT = Dep()
        k.dma("pool", kT_sb[:], keysT, writes=[dkT])
        x1 = k.sb(es, "x1", [128, 4, D], F32)
        dx1 = [Dep() for _ in range(4)]
        hnT = k.sb(es, "hnT", [128, NCH, 512], BF16)
        dhnT = Dep()
        s1p = k.sb(es, "s1p", [128, 4, 8, 128], F32)
        s2a = k.sb(es, "s2a", [128, 4, 8, 128], F32)
        ds12 = [Dep() for _ in range(4)]
        dgk = k.sb(es, "dgk", [128, 4, 8, 128], BF16)
        ddgk = [Dep() for _ in range(4)]

        for blk in range(NBLK):
            t0 = blk * 512
            with ExitStack() as ep:
                junk = k.sb(ep, "junk", [128, D], F32)
                gb = k.sb(ep, "gb", [128, D], F32)
                dgb = Dep()
                k.dma("sp", gb[:], lnf.partition_broadcast(128), writes=[dgb])
                hn = k.sb(ep, "hn", [128, D], BF16)
                ss = k.sb(ep, "ss", [128, 1], F32)
                dj, dhn, dss = Dep(), Dep(), Dep()
                wqc = [k.sb(ep, "wqc%d" % i, [128, NCH, 512], BF16) for i in range(2)]
                dwqc = [Dep(), Dep()]
                qT = k.sb(ep, "qT", [128, 16, 512], BF16)
                dqT = Dep()
                top = k.sb(ep, "top", [128, 8, 2, 16], F32)
                scr = k.sb(ep, "scr", [128, 256], F32)
                cand = k.sb(ep, "cand", [128, 8, 16, 16], F32)
                ctop = k.sb(ep, "ctop", [128, 8, 16], F32)
                sm = k.sb(ep, "sm", [128, 6, 8], F32)
                dtk = Dep()
                for t in range(4):
                    k.dma("sp", x1[:, t, :], xo[t0 + t * 128:t0 + (t + 1) * 128, :], reads=[d_xo], writes=[dx1[t]])
                    rms_tile(k, x1[:, t, :], dx1[t], gb[:], dgb, hn[:], dhn, junk[:], dj, ss[:], dss)
                    transpose_tile(k, hn, dhn, ident, did, pT, dpT,
                                   lambda c0, n: hnT[:, c0:c0 + n, t * 128:(t + 1) * 128], dhnT)
                for gq in range(4):
                    wc, dwc = wqc[gq % 2], dwqc[gq % 2]
                    k.dma("pool", wc[:], wq_r[:, :, gq * 512:(gq + 1) * 512], writes=[dwc])
                    for g4 in range(4):
                        g = gq * 4 + g4
                        pp, dpp = pF[g % 2], dpF[g % 2]
                        for c in range(NCH):
                            k.op("pe", lambda: nc.tensor.matmul(pp[:], wc[:, c, g4 * 128:(g4 + 1) * 128], hnT[:, c, :],
                                                                 start=(c == 0), stop=(c == NCH - 1)),
                                 reads=[dwc, dhnT], writes=[dpp], inc=(c == NCH - 1))
                        k.op("act", lambda: nc.scalar.copy(qT[:, g, :], pp[:]), reads=[dpp], writes=[dqT])
                for t in range(4):
                    for hh in range(4):
                        pp, dpp = pF[2 + hh % 2], dpF[2 + hh % 2]
                        for g4 in range(4):
                            g = hh * 4 + g4
                            k.op("pe", lambda: nc.tensor.matmul(pp[:, g4 * 128:(g4 + 1) * 128],
                                                                 qT[:, g, t * 128:(t + 1) * 128], kT_sb[:, g % 2, :],
                                                                 start=True, stop=True),
                                 reads=[dqT, dkT], writes=[dpp], inc=(g4 == 3))
                        ppv = pp[:].rearrange("p (h two k) -> p h two k", two=2, k=128)
                        k.op("act", lambda: nc.scalar.copy(s1p[:, t, hh * 2:hh * 2 + 2, :], ppv[:, :, 0, :]),
                             reads=[dpp], writes=[ds12[t]])
                        k.op("act", lambda: nc.scalar.copy(s2a[:, t, hh * 2:hh * 2 + 2, :], ppv[:, :, 1, :]),
                             reads=[dpp], writes=[ds12[t]])
                    for h in range(8):
                        for p2 in range(2):
                            src = (s1p if p2 == 0 else s2a)[:, t, h, :]
                            k.op("dve", lambda: nc.vector.max(out=top[:, h, p2, 0:8], in_=src),
                                 reads=[ds12[t]], writes=[dtk])
                            k.op("dve", lambda: nc.vector.match_replace(out=scr[:, 0:128], in_to_replace=top[:, h, p2, 0:8],
                                                                        in_values=src, imm_value=NEG),
                                 reads=[ds12[t], dtk], writes=[dtk])
                            k.op("dve", lambda: nc.vector.max(out=top[:, h, p2, 8:16], in_=scr[:, 0:128]),
                                 reads=[dtk], writes=[dtk])
                    tp = top[:].rearrange("p h two a -> p (h two a)")
                    in0 = bc(tp, [(32, 8), (1, 16), (0, 16)])
                    in1 = bass.AP(tp.tensor, tp.offset + 16, [list(tp.ap[0]), [32, 8], [0, 16], [1, 16]])
                    k.op("dve", lambda: nc.vector.tensor_tensor(cand[:], in0, in1, op=ALU.add), reads=[dtk], writes=[dtk])
                    for h in range(8):
                        ch = cand[:, h].rearrange("p a b -> p (a b)")
                        k.op("dve", lambda: nc.vector.max(out=ctop[:, h, 0:8], in_=ch), reads=[dtk], writes=[dtk])
                        k.op("dve", lambda: nc.vector.match_replace(out=scr[:], in_to_replace=ctop[:, h, 0:8],
                                                                    in_values=ch, imm_value=NEG),
                             reads=[dtk], writes=[dtk])
                        k.op("dve", lambda: nc.vector.max(out=ctop[:, h, 8:16], in_=scr[:]), reads=[dtk], writes=[dtk])
                    thr, mx, zz, kap, tmp = (sm[:, i, :] for i in range(5))
                    k.op("dve", lambda: nc.vector.tensor_scalar(thr, ctop[:, :, 15], -2e-5, None, op0=ALU.add),
                         reads=[dtk], writes=[dtk])
                    k.op("dve", lambda: nc.vector.tensor_copy(mx, ctop[:, :, 0]), reads=[dtk], writes=[dtk])
                    mxb = bc(sm[:, 1, :], [(1, 8), (0, 16)])
                    k.op("dve", lambda: nc.vector.tensor_tensor(ctop[:], ctop[:], mxb, op=ALU.subtract),
                         reads=[dtk], writes=[dtk])
                    k.op("act", lambda: nc.scalar.activation(ctop[:], ctop[:], AF.Exp), reads=[dtk], writes=[dtk])
                    k.op("dve", lambda: nc.vector.tensor_reduce(out=zz, in_=ctop[:], axis=AX.X, op=ALU.add),
                         reads=[dtk], writes=[dtk])
                    k.op("dve", lambda: nc.vector.tensor_tensor(tmp, thr, mx, op=ALU.subtract), reads=[dtk], writes=[dtk])
                    k.op("act", lambda: nc.scalar.activation(tmp, tmp, AF.Exp), reads=[dtk], writes=[dtk])
                    k.op("dve", lambda: nc.vector.reciprocal(zz, zz), reads=[dtk], writes=[dtk])
                    k.op("dve", lambda: nc.vector.tensor_tensor(kap, tmp, zz, op=ALU.mult), reads=[dtk], writes=[dtk])
                    thrb = bc(sm[:, 0, :], [(1, 8), (0, 128)])
                    k.op("dve", lambda: nc.vector.tensor_tensor(s1p[:, t], s1p[:, t], thrb, op=ALU.subtract),
                         reads=[dtk, ds12[t]], writes=[ds12[t]])
                    idb = bc(ident[:], [(0, 8), (1, 128)])
                    kpb = bc(sm[:, 3, :], [(1, 8), (0, 128)])
                    k.op("dve", lambda: nc.vector.tensor_tensor(dgk[:, t], idb, kpb, op=ALU.mult),
                         reads=[dtk, did], writes=[ddgk[t]])
            k.barrier()

            with ExitStack() as ee:
                ub = [k.sb(ee, "ub%d" % i, [128, NCH, 512], BF16) for i in range(2)]
                vb = [k.sb(ee, "vb%d" % i, [128, 4, D], BF16) for i in range(2)]
                dub, dvb = [Dep(), Dep()], [Dep(), Dep()]
                Tt = [k.sb(ee, "Tt0", [128, 2, 8, 128], F32)] * 2
                Et = [k.sb(ee, "Et%d" % i, [128, 2, 8, 128], F32) for i in range(2)]
                Mt = [k.sb(ee, "Mt%d" % i, [128, 2, 8, 128], BF16) for i in range(2)]
                dTt, dEt, dMt = [Dep()] * 2, [Dep(), Dep()], [Dep(), Dep()]
                Gs = [k.sb(ee, "Gs%d" % i, [128, 512], BF16) for i in range(2)]
                dGs = [Dep(), Dep()]
                AT = [k.sb(ee, "AT%d" % i, [128, 4, 512], BF16) for i in range(2)]
                dAT = [Dep(), Dep()]
                oi = 0
                for sbi in range(NSB):
                    u_, du_, v_, dv_ = ub[sbi % 2], dub[sbi % 2], vb[sbi % 2], dvb[sbi % 2]
                    k.dma("pool", u_[:], uT[sbi].rearrange("p (c e) -> p c e", e=512), writes=[du_])
                    k.dma("pool", v_[:], vsb[sbi], writes=[dv_])
                    A_, dA_ = AT[sbi % 2], dAT[sbi % 2]
                    for e4 in range(4):
                        i = sbi * 4 + e4
                        pG, dpG = pF[i % 2], dpF[i % 2]
                        pW, dpW = pF[2 + i % 2], dpF[2 + i % 2]
                        g_, dg_ = Gs[i % 2], dGs[i % 2]
                        for c in range(NCH):
                            k.op("pe", lambda: nc.tensor.matmul(pG[:], u_[:, c, e4 * 128:(e4 + 1) * 128], hnT[:, c, :],
                                                                 start=(c == 0), stop=(c == NCH - 1)),
                                 reads=[du_, dhnT], writes=[dpG], inc=(c == NCH - 1))
                        k.op("act", lambda: nc.scalar.activation(g_[:], pG[:], AF.Gelu), reads=[dpG], writes=[dg_])
                        for hf in range(2):
                            T_, E_, M_ = Tt[hf], Et[hf], Mt[hf]
                            s1v = s1p[:, 2 * hf:2 * hf + 2].rearrange("p t h i -> p (t h i)")
                            s1b = bass.AP(s1v.tensor, s1v.offset + i, [list(s1v.ap[0]), [128, 16], [0, 128]])
                            s2v = s2a[:, 2 * hf:2 * hf + 2].rearrange("p t h j -> p (t h) j")
                            k.op("pool", lambda: nc.gpsimd.tensor_tensor(T_[:].rearrange("p t h j -> p (t h) j"), s2v, s1b,
                                                                          op=ALU.add),
                                 reads=[ds12[2 * hf], ds12[2 * hf + 1]], writes=[dTt[hf]])
                            k.op("act", lambda: nc.scalar.activation(E_[:], T_[:], AF.Exp), reads=[dTt[hf]], writes=[dEt[hf]])
                            Ef = E_[:].rearrange("p t h j -> p (t h j)")
                            k.op("dve", lambda: nc.vector.scalar_tensor_tensor(M_[:].rearrange("p t h j -> p (t h j)"), Ef, 1.0, Ef,
                                                                               op0=ALU.is_ge, op1=ALU.mult),
                                 reads=[dEt[hf]], writes=[dMt[hf]])
                            for tt in range(2):
                                t = 2 * hf + tt
                                for h in range(8):
                                    k.op("pe", lambda: nc.tensor.matmul(pW[:, t * 128:(t + 1) * 128], M_[:, tt, h, :],
                                                                         dgk[:, t, h, :], start=(h == 0), stop=(h == 7)),
                                         reads=[dMt[hf], ddgk[t]], writes=[dpW], inc=(h == 7))
                        k.op("dve", lambda: nc.vector.tensor_tensor(A_[:, e4, :], g_[:], pW[:], op=ALU.mult),
                             reads=[dg_, dpW], writes=[dA_])
                    for t in range(4):
                        for cb in range(4):
                            pO, dpO = pF[4 + oi % 2], dpF[4 + oi % 2]
                            oi += 1
                            for e4 in range(4):
                                k.op("pe", lambda: nc.tensor.matmul(pO[:], A_[:, e4, t * 128:(t + 1) * 128],
                                                                     v_[:, e4, cb * 512:(cb + 1) * 512],
                                                                     start=(e4 == 0), stop=(e4 == 3)),
                                     reads=[dA_, dv_], writes=[dpO], inc=(e4 == 3))
                            xs = x1[:, t, cb * 512:(cb + 1) * 512]
                            k.op("dve", lambda: nc.vector.tensor_tensor(xs, xs, pO[:], op=ALU.add),
                                 reads=[dx1[t], dpO], writes=[dx1[t]])
            k.barrier()

            with ExitStack() as eo:
                if not final:
                    for t in range(4):
                        k.dma("sp", xo[t0 + t * 128:t0 + (t + 1) * 128, :], x1[:, t, :], reads=[dx1[t]], writes=[d_xo])
                if final or with_kv:
                    junk = k.sb(eo, "junk2", [128, D], F32)
                    gb2 = k.sb(eo, "gb2", [128, D], F32)
                    dgb2 = Dep()
                    k.dma("sp", gb2[:], (lnfin if final else kvn).partition_broadcast(128), writes=[dgb2])
                    ss = k.sb(eo, "ss2", [128, 1], F32)
                    dj, dss = Dep(), Dep()
                if final:
                    for t in range(4):
                        k.op("act", lambda: nc.scalar.activation(junk[:], x1[:, t, :], AF.Square, accum_out=ss[:]),
                             reads=[dx1[t]], writes=[dj, dss])
                        k.op("dve", lambda: nc.vector.tensor_scalar(ss[:], ss[:], 1.0 / D, EPS, op0=ALU.mult, op1=ALU.add),
                             reads=[dss], writes=[dss])
                        k.op("act", lambda: nc.scalar.activation(ss[:], ss[:], AF.Sqrt), reads=[dss], writes=[dss])
                        k.op("dve", lambda: nc.vector.reciprocal(ss[:], ss[:]), reads=[dss], writes=[dss])
                        k.op("dve", lambda: nc.vector.scalar_tensor_tensor(junk[:], x1[:, t, :], ss[:], gb2[:],
                                                                           op0=ALU.mult, op1=ALU.mult),
                             reads=[dx1[t], dss, dgb2], writes=[dj])
                        k.dma("sp", xo[t0 + t * 128:t0 + (t + 1) * 128, :], junk[:], reads=[dj], writes=[d_xo])
                if with_kv:
                    hn = k.sb(eo, "hnk", [128, D], BF16)
                    dhn = Dep()
                    wkc = [k.sb(eo, "wkc%d" % i, [128, NCH, 512], BF16) for i in range(2)]
                    dwkc = [Dep(), Dep()]
                    ko = [k.sb(eo, "ko%d" % i, [128, 512], BF16) for i in range(2)]
                    dko = [Dep(), Dep()]
                    wkv_r = wkv.rearrange("(c p) n -> p c n", p=128)
                    d_kTo, d_vo = Dep(), Dep()
                    for t in range(4):
                        rms_tile(k, x1[:, t, :], dx1[t], gb2[:], dgb2, hn[:], dhn, junk[:], dj, ss[:], dss)
                        transpose_tile(k, hn, dhn, ident, did, pT, dpT,
                                       lambda c0, n: hnT[:, c0:c0 + n, t * 128:(t + 1) * 128], dhnT)
                    oi = 0
                    for cq in range(8):
                        wc, dwc = wkc[cq % 2], dwkc[cq % 2]
                        k.dma("pool", wc[:], wkv_r[:, :, cq * 512:(cq + 1) * 512], writes=[dwc])
                        for j in range(4):
                            pp, dpp = pF[oi % 2], dpF[oi % 2]
                            o_, do_ = ko[oi % 2], dko[oi % 2]
                            oi += 1
                            if cq < 4:
                                for c in range(NCH):
                                    k.op("pe", lambda: nc.tensor.matmul(pp[:], wc[:, c, j * 128:(j + 1) * 128], hnT[:, c, :],
                                                                         start=(c == 0), stop=(c == NCH - 1)),
                                         reads=[dwc, dhnT], writes=[dpp], inc=(c == NCH - 1))
                                k.op("act", lambda: nc.scalar.copy(o_[:], pp[:]), reads=[dpp], writes=[do_])
                                f0 = cq * 512 + j * 128
                                k.dma("sp", kTo[f0:f0 + 128, t0:t0 + 512], o_[:], reads=[do_], writes=[d_kTo])
                            else:
                                for c in range(NCH):
                                    k.op("pe", lambda: nc.tensor.matmul(pp[:], hnT[:, c, j * 128:(j + 1) * 128], wc[:, c, :],
                                                                         start=(c == 0), stop=(c == NCH - 1)),
                                         reads=[dwc, dhnT], writes=[dpp], inc=(c == NCH - 1))
                                k.op("act", lambda: nc.scalar.copy(o_[:], pp[:]), reads=[dpp], writes=[do_])
                                k.dma("sp", vo[t0 + j * 128:t0 + (j + 1) * 128, (cq - 4) * 512:(cq - 3) * 512], o_[:],
                                      reads=[do_], writes=[d_vo])
            k.barrier()
        k.barrier()
    return nc


def build_ret(NB=8):
    S = NB * 512
    nc = bass.Bass("TRN2", target_bir_lowering=False)
    x = nc.dram_tensor("x", [S, D], F32, kind="ExternalInput").ap()
    lnm = nc.dram_tensor("lnm", [1, D], F32, kind="ExternalInput").ap()
    win = nc.dram_tensor("win", [D, 4, 1536], F32, kind="ExternalInput").ap()
    gng = nc.dram_tensor("gng", [1, 2048], F32, kind="ExternalInput").ap()
    cosT = nc.dram_tensor("cosT", [128, S], F32, kind="ExternalInput").ap()
    sinT = nc.dram_tensor("sinT", [128, S], F32, kind="ExternalInput").ap()
    dmk = nc.dram_tensor("dmk", [128, 4, 128], F32, kind="ExternalInput").ap()
    xib = nc.dram_tensor("xib", [128, 4, 128], F32, kind="ExternalInput").ap()
    zc = nc.dram_tensor("zc", [128, 8], F32, kind="ExternalInput").ap()
    yT = nc.dram_tensor("yT", [2048, S], BF16, kind="ExternalOutput").ap()
    win_r = win.rearrange("(c p) h n -> p c h n", p=128)

    with ExitStack() as es:
        k = KB(nc, es)
        ident, did = make_ident(k, es)
        pT = [k.ps(es, "pT%d" % i, [128, 8, 128], BF16) for i in range(2)]
        dpT = [Dep() for _ in range(2)]
        pF = [k.ps(es, "pF%d" % i, [128, 512], F32) for i in range(6)]
        dpF = [Dep() for _ in range(6)]
        gb = k.sb(es, "gb", [128, D], F32); dgb = Dep()
        k.dma("sp", gb[:], lnm.partition_broadcast(128), writes=[dgb])
        gg = k.sb(es, "gg", [128, 2048], F32); dgg = Dep()
        k.dma("sp", gg[:], gng.partition_broadcast(128), writes=[dgg])
        dm = k.sb(es, "dm", [128, 4, 128], F32); ddm = Dep()
        k.dma("sp", dm[:], dmk, writes=[ddm])
        xi = k.sb(es, "xi", [128, 4, 128], F32); dxi = Dep()
        k.dma("sp", xi[:], xib, writes=[dxi])
        zz = k.sb(es, "zz", [128, 8], F32); dzz = Dep()
        k.dma("sp", zz[:], zc, writes=[dzz])
        R = k.sb(es, "R", [128, 4, 2, 512], F32)
        Rb = k.sb(es, "Rb", [128, 4, 2, 512], BF16)
        dR = [Dep() for _ in range(4)]
        dRb = [Dep() for _ in range(4)]
        for h in range(4):
            k.op("pool", lambda: nc.gpsimd.memset(R[:, h], 0.0), writes=[dR[h]])
            k.op("pool", lambda: nc.gpsimd.memset(Rb[:, h], 0.0), writes=[dRb[h]])
        xt = k.sb(es, "xt", [128, D], F32); dxt = Dep()
        hn = k.sb(es, "hn", [128, D], BF16); dhn = Dep()
        junk, dj = hn, dhn
        ss = k.sb(es, "ss", [128, 1], F32); dss = Dep()
        hnT = k.sb(es, "hnT", [128, NCH, 512], BF16); dhnT = Dep()
        wh = [k.sb(es, "wh%d" % i, [128, NCH, 1536], BF16) for i in range(2)]
        dwh = [Dep(), Dep()]
        cs = k.sb(es, "cs", [128, 2, 512], F32); dcs = Dep()
        raw = k.sb(es, "raw", [128, 2, 2, 512], F32); draw = [Dep(), Dep()]
        ta = k.sb(es, "ta", [128, 512], F32); tb_ = k.sb(es, "tb", [128, 512], F32)
        dta, dtb = Dep(), Dep()
        qT = k.sb(es, "qT", [128, 2, 512], BF16); kT = k.sb(es, "kT", [128, 2, 512], BF16)
        qx = k.sb(es, "qx", [128, 2, 512], BF16)
        dqT, dkT, dqx = Dep(), Dep(), Dep()
        vt = k.sb(es, "vt", [128, 4, 512], BF16); sg = k.sb(es, "sg", [128, 4, 512], BF16)
        dvt, dsg = Dep(), Dep()
        kz = k.sb(es, "kz", [128, 256], BF16); dkz = Dep()
        sc = k.sb(es, "sc", [128, 128], BF16); dsc = Dep()
        st = k.sb(es, "st", [128, 16], F32); dst = Dep()
        yn = k.sb(es, "yn", [128, 512], F32); dyn = Dep()
        yb = k.sb(es, "yb", [128, 512], BF16); dyb = Dep()
        yTs = [k.sb(es, "yTs0", [128, 4, 512], BF16)] * 2
        dyTs = [Dep()] * 2
        d_out = Dep()
        wi = 0
        for blk in range(NB):
            t0 = blk * 512
            k.dma("sp", cs[:, 0, :], cosT[:, t0:t0 + 512], writes=[dcs])
            k.dma("sp", cs[:, 1, :], sinT[:, t0:t0 + 512], writes=[dcs])
            for t in range(4):
                k.dma("sp", xt[:], x[t0 + t * 128:t0 + (t + 1) * 128, :], writes=[dxt])
                rms_tile(k, xt[:], dxt, gb[:], dgb, hn[:], dhn, junk[:], dj, ss[:], dss)
                transpose_tile(k, hn, dhn, ident, did, pT, dpT,
                               lambda c0, n: hnT[:, c0:c0 + n, t * 128:(t + 1) * 128], dhnT)
            for h in range(4):
                w_, dw_ = wh[wi % 2], dwh[wi % 2]
                ys, dys = yTs[wi % 2], dyTs[wi % 2]
                wi += 1
                k.dma("pool", w_[:], win_r[:, :, h, :], writes=[dw_])
                for which in range(2):
                    for n in range(2):
                        pp, dpp = pF[n], dpF[n]
                        off = which * 256 + n * 128
                        for c in range(NCH):
                            k.op("pe", lambda: nc.tensor.matmul(pp[:], w_[:, c, off:off + 128], hnT[:, c, :],
                                                                 start=(c == 0), stop=(c == NCH - 1)),
                                 reads=[dw_, dhnT], writes=[dpp], inc=(c == NCH - 1))
                        k.op("act", lambda: nc.scalar.copy(raw[:, which, n, :], pp[:]), reads=[dpp], writes=[draw[which]])
                    t1, t2 = raw[:, which, 0, :], raw[:, which, 1, :]
                    dst_, ddst_ = (qT, dqT) if which == 0 else (kT, dkT)
                    dr = draw[which]
                    k.op("pool", lambda: nc.gpsimd.tensor_tensor(ta[:], t1, cs[:, 0, :], op=ALU.mult), reads=[dr, dcs], writes=[dta])
                    k.op("pool", lambda: nc.gpsimd.tensor_tensor(tb_[:], t2, cs[:, 1, :], op=ALU.mult), reads=[dr, dcs], writes=[dtb])
                    k.op("dve", lambda: nc.vector.tensor_tensor(dst_[:, 0, :], ta[:], tb_[:], op=ALU.subtract),
                         reads=[dta, dtb], writes=[ddst_])
                    k.op("pool", lambda: nc.gpsimd.tensor_tensor(ta[:], t1, cs[:, 1, :], op=ALU.mult), reads=[dr, dcs], writes=[dta])
                    k.op("pool", lambda: nc.gpsimd.tensor_tensor(tb_[:], t2, cs[:, 0, :], op=ALU.mult), reads=[dr, dcs], writes=[dtb])
                    k.op("dve", lambda: nc.vector.tensor_tensor(dst_[:, 1, :], ta[:], tb_[:], op=ALU.add),
                         reads=[dta, dtb], writes=[ddst_])
                xv = xi[:, h, :]
                xbc = bass.AP(xv.tensor, xv.offset, [list(xv.ap[0]), [0, 8], [1, 128]])
                k.op("dve", lambda: nc.vector.tensor_tensor(qx[:].rearrange("p n (c i) -> p (n c) i", i=128),
                                                            qT[:].rearrange("p n (c i) -> p (n c) i", i=128), xbc, op=ALU.mult),
                     reads=[dqT, dxi], writes=[dqx])
                for t in range(4):
                    for which in range(2):
                        pp, dpp = pF[2 + which], dpF[2 + which]
                        off = 512 + which * 512
                        for c in range(NCH):
                            k.op("pe", lambda: nc.tensor.matmul(pp[:], hnT[:, c, t * 128:(t + 1) * 128], w_[:, c, off:off + 512],
                                                                 start=(c == 0), stop=(c == NCH - 1)),
                                 reads=[dw_, dhnT], writes=[dpp], inc=(c == NCH - 1))
                        if which == 0:
                            k.op("act", lambda: nc.scalar.copy(vt[:, t, :], pp[:]), reads=[dpp], writes=[dvt])
                        else:
                            k.op("act", lambda: nc.scalar.activation(sg[:, t, :], pp[:], AF.Silu), reads=[dpp], writes=[dsg])
                for t in range(4):
                    cl = slice(t * 128, (t + 1) * 128)
                    p_, dp_ = pT[t % 2], dpT[t % 2]
                    for n in range(2):
                        k.op("pe", lambda: nc.tensor.transpose(p_[:, n, :], kT[:, n, cl], ident[:]),
                             reads=[dkT, did], writes=[dp_])
                    k.op("dve", lambda: nc.vector.tensor_scalar(kz[:].rearrange("p (n d) -> p n d", n=2), p_[:, 0:2, :],
                                                                zz[:, h:h + 1], None, op0=ALU.mult),
                         reads=[dp_, dzz], writes=[dkz])
                    pS, dpS = pF[4], dpF[4]
                    for n in range(2):
                        k.op("pe", lambda: nc.tensor.matmul(pS[:, 0:128], kT[:, n, cl], qT[:, n, cl], start=(n == 0), stop=(n == 1)),
                             reads=[dkT, dqT], writes=[dpS], inc=(n == 1))
                    k.op("dve", lambda: nc.vector.tensor_tensor(sc[:], pS[:, 0:128], dm[:, h, :], op=ALU.mult),
                         reads=[dpS, ddm], writes=[dsc])
                    pO, dpO = pF[5], dpF[5]
                    k.op("pe", lambda: nc.tensor.matmul(pO[:], sc[:], vt[:, t, :], start=True, stop=False),
                         reads=[dsc, dvt], writes=[dpO], inc=False)
                    for n in range(2):
                        k.op("pe", lambda: nc.tensor.matmul(pO[:], qx[:, n, cl], Rb[:, h, n, :], start=False, stop=(n == 1)),
                             reads=[dqx, dRb[h]], writes=[dpO], inc=(n == 1))
                    for n in range(2):
                        pR, dpR = pF[n], dpF[n]
                        k.op("pe", lambda: nc.tensor.matmul(pR[:], kz[:, n * 128:(n + 1) * 128], vt[:, t, :], start=True, stop=True),
                             reads=[dkz, dvt], writes=[dpR])
                        k.op("dve", lambda: nc.vector.scalar_tensor_tensor(R[:, h, n, :], R[:, h, n, :], zz[:, 4 + h:5 + h], pR[:],
                                                                           op0=ALU.mult, op1=ALU.add),
                             reads=[dpR, dzz, dR[h]], writes=[dR[h]])
                    k.op("act", lambda: nc.scalar.copy(Rb[:, h], R[:, h]), reads=[dR[h]], writes=[dRb[h]])
                    k.op("dve", lambda: nc.vector.bn_stats(st[:, 0:6], pO[:]), reads=[dpO], writes=[dst])
                    k.op("dve", lambda: nc.vector.bn_aggr(st[:, 6:8], st[:, 0:6]), reads=[dst], writes=[dst])
                    k.op("dve", lambda: nc.vector.tensor_scalar(st[:, 8:9], st[:, 7:8], EPS, None, op0=ALU.add),
                         reads=[dst], writes=[dst])
                    k.op("act", lambda: nc.scalar.activation(st[:, 8:9], st[:, 8:9], AF.Sqrt), reads=[dst], writes=[dst])
                    k.op("dve", lambda: nc.vector.reciprocal(st[:, 8:9], st[:, 8:9]), reads=[dst], writes=[dst])
                    k.op("dve", lambda: nc.vector.scalar_tensor_tensor(st[:, 9:10], st[:, 6:7], -1.0, st[:, 8:9],
                                                                       op0=ALU.mult, op1=ALU.mult),
                         reads=[dst], writes=[dst])
                    k.op("act", lambda: nc.scalar.activation(yn[:], pO[:], AF.Identity, bias=st[:, 9:10], scale=st[:, 8:9]),
                         reads=[dpO, dst], writes=[dyn])
                    k.op("pool", lambda: nc.gpsimd.tensor_tensor(yn[:], yn[:], gg[:, h * 512:(h + 1) * 512], op=ALU.mult),
                         reads=[dyn, dgg], writes=[dyn])
                    k.op("dve", lambda: nc.vector.tensor_tensor(yb[:], yn[:], sg[:, t, :], op=ALU.mult),
                         reads=[dyn, dsg], writes=[dyb])
                    p2, dp2 = pT[(t + 1) % 2], dpT[(t + 1) % 2]
                    for f in range(4):
                        k.op("pe", lambda: nc.tensor.transpose(p2[:, 4 + f, :], yb[:, f * 128:(f + 1) * 128], ident[:]),
                             reads=[dyb, did], writes=[dp2])
                    k.op("act", lambda: nc.scalar.copy(ys[:, :, cl], p2[:, 4:8, :]), reads=[dp2], writes=[dys])
                k.dma("sp", yT[h * 512:(h + 1) * 512, t0:t0 + 512].rearrange("(f p) t -> p f t", p=128), ys[:],
                      reads=[dys], writes=[d_out])
        k.barrier()
    return nc


def ret_consts(HG, S):
    f = np.float32
    pos = np.arange(S, dtype=f)
    inv = (1.0 / (np.float32(10000.0) ** (np.arange(128, dtype=f) / np.float32(128)))).astype(f)
    ang = (pos[:, None] * inv[None, :]).astype(f)
    cosT = np.ascontiguousarray(np.cos(ang.astype(np.float64)).T).astype(f)
    sinT = np.ascontiguousarray(np.sin(ang.astype(np.float64)).T).astype(f)
    dmk = np.zeros((128, 4, 128), f)
    xib = np.zeros((128, 4, 128), f)
    zc = np.zeros((128, 8), f)
    idx = np.arange(128, dtype=np.float64)
    for hl in range(4):
        h = HG * 4 + hl
        lg = np.log1p(-np.exp2(-5.0 - h))
        diff = idx[None, :] - idx[:, None]
        dmk[:, hl, :] = np.where(diff >= 0, np.exp(lg * np.maximum(diff, 0)), 0.0) / 16.0
        xib[:, hl, :] = np.exp(lg * (idx + 1.0))[None, :]
        zc[:, hl] = np.exp(lg * (127.0 - idx)) / 16.0
        zc[:, 4 + hl] = np.exp(lg * 128.0)
    return cosT, sinT, dmk, xib, zc


def ret_inputs(xb, lnm, w_in, gn, HG, S):
    cosT, sinT, dmk, xib, zc = ret_consts(HG, S)
    cols = []
    for hl in range(4):
        h = HG * 4 + hl
        cols.append(np.concatenate([w_in[:, h * 256:(h + 1) * 256], w_in[:, 2048 + h * 256:2048 + (h + 1) * 256],
                                    w_in[:, 4096 + h * 512:4096 + (h + 1) * 512],
                                    w_in[:, 8192 + h * 512:8192 + (h + 1) * 512]], axis=1))
    win = np.ascontiguousarray(np.stack(cols, axis=1))
    return {"x": np.ascontiguousarray(xb), "lnm": np.ascontiguousarray(lnm.reshape(1, D)), "win": win,
            "gng": np.ascontiguousarray(gn[HG * 2048:(HG + 1) * 2048].reshape(1, 2048)),
            "cosT": cosT, "sinT": sinT, "dmk": dmk, "xib": xib, "zc": zc}


def build_att(NB=8):
    S = NB * 512
    nc = bass.Bass("TRN2", target_bir_lowering=False)
    x = nc.dram_tensor("x", [S, D], F32, kind="ExternalInput").ap()
    lnm = nc.dram_tensor("lnm", [1, D], F32, kind="ExternalInput").ap()
    wq8 = nc.dram_tensor("wq8", [D, 1024], F32, kind="ExternalInput").ap()
    kT8 = nc.dram_tensor("kT8", [1024, S], BF16, kind="ExternalInput").ap()
    v8 = nc.dram_tensor("v8", [S, 1024], BF16, kind="ExternalInput").ap()
    bias8 = nc.dram_tensor("bias8", [128, 8, 640], F32, kind="ExternalInput").ap()
    oT = nc.dram_tensor("oT", [1024, S], BF16, kind="ExternalOutput").ap()
    kT_r = kT8.rearrange("(h p) t -> p h t", p=128)
    v_r = v8.rearrange("(j p) f -> p j f", p=128)
    oT_r = oT.rearrange("(h p) t -> p h t", p=128)

    with ExitStack() as es:
        k = KB(nc, es)
        ident, did = make_ident(k, es)
        pT = [k.ps(es, "pT%d" % i, [128, 8, 128], BF16) for i in range(2)]
        dpT = [Dep() for _ in range(2)]
        pF = [k.ps(es, "pF%d" % i, [128, 512], F32) for i in range(2)]
        dpF = [Dep() for _ in range(2)]
        pS = [k.ps(es, "pS%d" % i, [128, 1024], F32) for i in range(2)]
        dpS = [Dep() for _ in range(2)]
        gb = k.sb(es, "gb", [128, D], F32); dgb = Dep()
        k.dma("sp", gb[:], lnm.partition_broadcast(128), writes=[dgb])
        bs = k.sb(es, "bs", [128, 8, 640], F32); dbs = Dep()
        k.dma("sp", bs[:], bias8, writes=[dbs])
        wq = k.sb(es, "wq", [128, NCH, 1024], BF16); dwq = Dep()
        k.dma("pool", wq[:], wq8.rearrange("(c p) n -> p c n", p=128), writes=[dwq])
        xt = k.sb(es, "xt", [128, D], F32); dxt = Dep()
        hn = k.sb(es, "hn", [128, D], BF16); dhn = Dep()
        ss = k.sb(es, "ss", [128, 1], F32); dss = Dep()
        hnT = k.sb(es, "hnT", [128, NCH, 512], BF16); dhnT = Dep()
        qT = k.sb(es, "qT", [128, 8, 512], BF16); dqT = Dep()
        kw = k.sb(es, "kw", [128, 8, 1024], BF16); dkw = Dep()
        vw = k.sb(es, "vw", [128, 8, 1024], BF16); dvw = Dep()
        sS = k.sb(es, "sS", [128, 640], F32); dsS = Dep()
        Pf = k.sb(es, "Pf", [128, 640], BF16); dPf = Dep()
        Pn = k.sb(es, "Pn", [128, 640], BF16); dPn = Dep()
        PT = k.sb(es, "PT", [128, 5, 128], BF16); dPT = Dep()
        sm = k.sb(es, "sm", [128, 4], F32); dsm = Dep()
        oTs = k.sb(es, "oTs", [128, 8, 512], BF16); doTs = Dep()
        d_out = Dep()
        it = 0
        for blk in range(NB):
            t0 = blk * 512
            for t in range(4):
                k.dma("sp", xt[:], x[t0 + t * 128:t0 + (t + 1) * 128, :], writes=[dxt])
                rms_tile(k, xt[:], dxt, gb[:], dgb, hn[:], dhn, hn[:], dhn, ss[:], dss)
                transpose_tile(k, hn, dhn, ident, did, pT, dpT,
                               lambda c0, n: hnT[:, c0:c0 + n, t * 128:(t + 1) * 128], dhnT)
            for h in range(8):
                pp, dpp = pF[h % 2], dpF[h % 2]
                for c in range(NCH):
                    k.op("pe", lambda: nc.tensor.matmul(pp[:], wq[:, c, h * 128:(h + 1) * 128], hnT[:, c, :],
                                                         start=(c == 0), stop=(c == NCH - 1)),
                         reads=[dwq, dhnT], writes=[dpp], inc=(c == NCH - 1))
                k.op("act", lambda: nc.scalar.mul(qT[:, h, :], pp[:], 128.0 ** -0.5), reads=[dpp], writes=[dqT])
            lo = 0 if blk > 0 else 512
            k.dma("sp", kw[:, :, lo:1024], kT_r[:, :, t0 - 512 + lo:t0 + 512], writes=[dkw])
            k.dma("sp", vw[:, lo // 128:8, :], v_r[:, (t0 - 512 + lo) // 128:(t0 + 512) // 128, :], writes=[dvw])
            for t in range(4):
                jlo = max(t, 4 - 4 * blk) if blk == 0 else t
                nj = t + 4 - jlo + 1
                ncol = nj * 128
                b0 = (jlo - t) * 128
                for h in range(8):
                    ps_, dps_ = pS[it % 2], dpS[it % 2]
                    po_, dpo_ = pF[it % 2], dpF[it % 2]
                    pt_, dpt_ = pT[it % 2], dpT[it % 2]
                    it += 1
                    n1 = min(ncol, 512)
                    k.op("pe", lambda: nc.tensor.matmul(ps_[:, 0:n1], qT[:, h, t * 128:(t + 1) * 128],
                                                         kw[:, h, jlo * 128:jlo * 128 + n1], start=True, stop=True),
                         reads=[dqT, dkw], writes=[dps_], inc=(ncol <= 512))
                    if ncol > 512:
                        k.op("pe", lambda: nc.tensor.matmul(ps_[:, 512:640], qT[:, h, t * 128:(t + 1) * 128],
                                                             kw[:, h, jlo * 128 + 512:jlo * 128 + 640], start=True, stop=True),
                             reads=[dqT, dkw], writes=[dps_])
                    k.op("dve", lambda: nc.vector.tensor_tensor(sS[:, 0:ncol], ps_[:, 0:ncol], bs[:, h, b0:b0 + ncol], op=ALU.add),
                         reads=[dps_, dbs], writes=[dsS])
                    k.op("dve", lambda: nc.vector.tensor_reduce(out=sm[:, 0:1], in_=sS[:, 0:ncol], axis=AX.X, op=ALU.max, negate=True),
                         reads=[dsS], writes=[dsm])
                    k.op("act", lambda: nc.scalar.activation(Pf[:, 0:ncol], sS[:, 0:ncol], AF.Exp, bias=sm[:, 0:1], scale=1.0,
                                                             accum_out=sm[:, 1:2]),
                         reads=[dsS, dsm], writes=[dPf, dsm])
                    k.op("dve", lambda: nc.vector.reciprocal(sm[:, 2:3], sm[:, 1:2]), reads=[dsm], writes=[dsm])
                    k.op("dve", lambda: nc.vector.tensor_scalar(Pn[:, 0:ncol], Pf[:, 0:ncol], sm[:, 2:3], None, op0=ALU.mult),
                         reads=[dPf, dsm], writes=[dPn])
                    for jj in range(nj):
                        k.op("pe", lambda: nc.tensor.transpose(pt_[:, jj, :], Pn[:, jj * 128:(jj + 1) * 128], ident[:]),
                             reads=[dPn, did], writes=[dpt_])
                    k.op("act", lambda: nc.scalar.copy(PT[:, 0:nj, :], pt_[:, 0:nj, :]), reads=[dpt_], writes=[dPT])
                    for jj in range(nj):
                        k.op("pe", lambda: nc.tensor.matmul(po_[:, 0:128], vw[:, jlo + jj, h * 128:(h + 1) * 128], PT[:, jj, :],
                                                             start=(jj == 0), stop=(jj == nj - 1)),
                             reads=[dvw, dPT], writes=[dpo_], inc=(jj == nj - 1))
                    k.op("act", lambda: nc.scalar.copy(oTs[:, h, t * 128:(t + 1) * 128], po_[:, 0:128]),
                         reads=[dpo_], writes=[doTs])
            k.dma("sp", oT_r[:, :, t0:t0 + 512], oTs[:], reads=[doTs], writes=[d_out])
        k.barrier()
    return nc


def att_bias(rel_bias8):
    q = np.arange(128)[:, None]
    col = np.arange(640)[None, :]
    idx = np.clip(512 + q - col, -128, 128) + 128
    allowed = np.where(q < 64, col <= 575, col >= 64)
    b = rel_bias8[:, idx]
    b = np.where(allowed[None], b, np.float32(NEG)).astype(np.float32)
    return np.ascontiguousarray(b.transpose(1, 0, 2))


_PROGS = {}


def _prog(key, fn):
    if key not in _PROGS:
        _PROGS[key] = fn()
    return _PROGS[key]


def kernel(x, ln_mix, ln_ffn, ret_w_in, ret_w_out, ret_gn, kv_norm, w_kv, att_w_q, att_w_o, att_rel_bias,
           peer_w_q, peer_sub_keys, peer_u, peer_v, ln_final):
    f = np.float32
    A = lambda a: np.asarray(a, dtype=f)
    x, ln_mix, ln_ffn, ret_w_in, ret_w_out, ret_gn = A(x), A(ln_mix), A(ln_ffn), A(ret_w_in), A(ret_w_out), A(ret_gn)
    kv_norm, w_kv, att_w_q, att_w_o, att_rel_bias = A(kv_norm), A(w_kv), A(att_w_q), A(att_w_o), A(att_rel_bias)
    peer_w_q, peer_sub_keys, peer_u, peer_v, ln_final = A(peer_w_q), A(peer_sub_keys), A(peer_u), A(peer_v), A(ln_final)
    B, S = 4, 4096
    cores = list(range(8))
    xs = np.ascontiguousarray(x.reshape(B, S, D))
    kT_full = v_full = None
    for l in range(4):
        if l < 2:
            nc = _prog("ret", lambda: build_ret(8))
            in_maps = [ret_inputs(xs[c // 2], ln_mix[l], ret_w_in[l], ret_gn[l], c % 2, S) for c in cores]
            res = run_bass_kernel_spmd(nc, in_maps, core_ids=cores).results
            ym = [r["yT"] for r in res]
            KY, wo = 4096, ret_w_out[l]
        else:
            j = l - 2
            nc = _prog("att", lambda: build_att(8))
            in_maps = []
            for c in cores:
                b, hg = c // 2, c % 2
                in_maps.append({"x": xs[b], "lnm": ln_mix[l].reshape(1, D),
                                "wq8": np.ascontiguousarray(att_w_q[j][:, hg * 1024:(hg + 1) * 1024]),
                                "kT8": np.ascontiguousarray(kT_full[b][hg * 1024:(hg + 1) * 1024]),
                                "v8": np.ascontiguousarray(v_full[b][:, hg * 1024:(hg + 1) * 1024]),
                                "bias8": att_bias(att_rel_bias[j][hg * 8:(hg + 1) * 8])})
            res = run_bass_kernel_spmd(nc, in_maps, core_ids=cores).results
            ym = [r["oT"] for r in res]
            KY, wo = 2048, att_w_o[j]
        with_kv, final = (l == 1), (l == 3)
        nc = _prog(("post", KY, with_kv, final), lambda: build_post(KY, with_kv, final))
        keysT = np.ascontiguousarray(peer_sub_keys[l].transpose(2, 0, 1))
        uTb = np.ascontiguousarray(peer_u[l].reshape(32, 512, 16, 128).transpose(0, 3, 2, 1)).reshape(32, 128, 16 * 512)
        in_maps = []
        for c in cores:
            b, half = c // 2, c % 2
            ts = slice(half * 2048, (half + 1) * 2048)
            yT = np.ascontiguousarray(np.concatenate([ym[2 * b][:, ts], ym[2 * b + 1][:, ts]], axis=0))
            m = {"x": np.ascontiguousarray(xs[b, ts]), "yT": yT, "wo": wo, "lnf": ln_ffn[l].reshape(1, D),
                 "wq": peer_w_q[l], "keysT": keysT, "uT": uTb, "vv": peer_v[l]}
            if with_kv:
                m["kvn"] = kv_norm.reshape(1, D)
                m["wkv"] = w_kv
            if final:
                m["lnfin"] = ln_final.reshape(1, D)
            in_maps.append(m)
        res = run_bass_kernel_spmd(nc, in_maps, core_ids=cores).results
        xs = np.ascontiguousarray(np.stack([np.concatenate([res[2 * b]["xo"], res[2 * b + 1]["xo"]], axis=0)
                                            for b in range(B)]))
        if with_kv:
            kT_full = [np.concatenate([res[2 * b]["kTo"], res[2 * b + 1]["kTo"]], axis=1) for b in range(B)]
            v_full = [np.concatenate([res[2 * b]["vo"], res[2 * b + 1]["vo"]], axis=0) for b in range(B)]
    return xs.reshape(B, S, D).astype(f)
```
